# Optimizing a Trainium2 kernel written in Bass

```python
import jax, jax.numpy as jnp
from jax import lax
import numpy as np

D_MODEL = 1024
BATCH = 4
SEQ = 4096
DEPTH = 4

N_MIXERS = 3
RMS_EPS = 1e-6
BLOCK = 128
NEG = -1e30

SWA_HEAD_DIM = 64
SWA_HEADS = D_MODEL // SWA_HEAD_DIM
SWA_KV_HEADS = SWA_HEADS // 8
SWA_GROUP = SWA_HEADS // SWA_KV_HEADS
WINDOW = 128

SB_HEAD_DIM = 64
SB_HEADS = D_MODEL // SB_HEAD_DIM

RET_QK_DIM = 256
RET_HEADS = D_MODEL // RET_QK_DIM
RET_V_DIM = 2 * RET_QK_DIM

D_FF = 4 * D_MODEL

N_A = len(range(0, DEPTH, N_MIXERS))
N_B = len(range(1, DEPTH, N_MIXERS))
N_C = len(range(2, DEPTH, N_MIXERS))

kernel_name = "hybrid_swa_stickbreak_retnet_trunk"


def rms_norm(x, g):
    xf = x.astype(jnp.float32)
    y = xf * lax.rsqrt(jnp.mean(xf * xf, axis=-1, keepdims=True) + RMS_EPS)
    return (y * g.astype(jnp.float32)).astype(x.dtype)


def alibi_slopes(n_heads):
    return jnp.exp2(-8.0 * jnp.arange(1, n_heads + 1, dtype=jnp.float32) / n_heads)


def swa_mixer(h, w_qkv, sinks, w_o):
    B_, T, _ = h.shape
    nb = T // BLOCK
    HD, KV, G = SWA_HEAD_DIM, SWA_KV_HEADS, SWA_GROUP
    qkv = h @ w_qkv
    q, k, v = jnp.split(qkv, [SWA_HEADS * HD, (SWA_HEADS + KV) * HD], axis=-1)
    q = q.reshape(B_, nb, BLOCK, KV, G, HD)
    k = k.reshape(B_, nb, BLOCK, KV, HD)
    v = v.reshape(B_, nb, BLOCK, KV, HD)

    def with_prev(a):
        prev = jnp.pad(a, ((0, 0), (1, 0), (0, 0), (0, 0), (0, 0)))[:, :-1]
        return jnp.concatenate([prev, a], axis=2)

    kb, vb = with_prev(k), with_prev(v)
    s = jnp.einsum('bnqkgd,bnskd->bnkgqs', q, kb).astype(jnp.float32) * (HD ** -0.5)

    blk = jnp.arange(nb)[:, None, None]
    qi = jnp.arange(BLOCK)[None, :, None]
    kj = jnp.arange(2 * BLOCK)[None, None, :]
    dist = BLOCK + qi - kj
    kpos = (blk - 1) * BLOCK + kj
    mask = (dist >= 0) & (dist < WINDOW) & (kpos >= 0)
    slopes = alibi_slopes(SWA_HEADS).reshape(1, 1, KV, G, 1, 1)
    bias = -slopes * dist.astype(jnp.float32)[None, :, None, None]
    logits = jnp.where(mask[None, :, None, None], s + bias, NEG)

    sink = sinks.astype(jnp.float32).reshape(1, 1, KV, G, 1, 1)
    m = jnp.maximum(jnp.max(logits, axis=-1, keepdims=True), sink)
    p = jnp.exp(logits - m)
    probs = p / (jnp.sum(p, axis=-1, keepdims=True) + jnp.exp(sink - m))
    o = jnp.einsum('bnkgqs,bnskd->bnqkgd', probs.astype(vb.dtype), vb)
    return o.reshape(B_, T, SWA_HEADS * HD) @ w_o


def stick_breaking_mixer(h, w_qkv, w_o):
    B_, T, _ = h.shape
    HD, H = SB_HEAD_DIM, SB_HEADS
    qkv = (h @ w_qkv).reshape(B_, T, 3, H, HD)
    q = jnp.transpose(qkv[:, :, 0], (0, 2, 1, 3))
    k = jnp.transpose(qkv[:, :, 1], (0, 2, 1, 3))
    v = jnp.transpose(qkv[:, :, 2], (0, 2, 1, 3))
    scale = HD ** -0.5
    outs = []
    for n in range(T // BLOCK):
        q0, end = n * BLOCK, (n + 1) * BLOCK
        z = jnp.einsum('bhqd,bhsd->bhqs', q[:, :, q0:end], k[:, :, :end]).astype(jnp.float32) * scale
        t_pos = q0 + jnp.arange(BLOCK)[:, None]
        s_pos = jnp.arange(end)[None, :]
        causal = (s_pos < t_pos)[None, None]
        log_beta = jax.nn.log_sigmoid(z)
        log_1m_beta = jnp.where(causal, jax.nn.log_sigmoid(-z), 0.0)
        rest = lax.cumsum(log_1m_beta, axis=3, reverse=True) - log_1m_beta
        a = jnp.where(causal, jnp.exp(log_beta + rest), 0.0)
        outs.append(jnp.einsum('bhqs,bhsd->bhqd', a.astype(v.dtype), v[:, :, :end]))
    o = jnp.concatenate(outs, axis=2)
    return jnp.transpose(o, (0, 2, 1, 3)).reshape(B_, T, H * HD) @ w_o


def retention_mixer(h, w_in, w_o):
    B_, T, _ = h.shape
    nc, C, H, dk, dv = T // BLOCK, BLOCK, RET_HEADS, RET_QK_DIM, RET_V_DIM
    proj = h @ w_in
    q, k, v, g = jnp.split(proj, [H * dk, 2 * H * dk, 2 * H * dk + H * dv], axis=-1)
    q = q.astype(jnp.float32).reshape(B_, nc, C, H, dk)
    k = k.astype(jnp.float32).reshape(B_, nc, C, H, dk) * (dk ** -0.5)
    v = v.astype(jnp.float32).reshape(B_, nc, C, H, dv)

    log_gamma = jnp.log1p(-jnp.exp2(-5.0 - jnp.arange(H, dtype=jnp.float32)))
    idx = jnp.arange(C)
    diff = idx[:, None] - idx[None, :]
    decay_mat = jnp.where(diff >= 0,
                          jnp.exp(log_gamma[:, None, None] * jnp.maximum(diff, 0).astype(jnp.float32)),
                          0.0)
    scores = jnp.einsum('bnihd,bnjhd->bnhij', q, k) * decay_mat
    o_inner = jnp.einsum('bnhij,bnjhe->bnihe', scores, v)
    q_decay = jnp.exp(log_gamma[:, None] * (idx + 1).astype(jnp.float32))
    k_decay = jnp.exp(log_gamma[:, None] * (C - 1 - idx).astype(jnp.float32))
    chunk_decay = jnp.exp(log_gamma * C)[None, :, None, None]
    kv_chunk = jnp.einsum('bnjhd,hj,bnjhe->bnhde', k, k_decay, v)

    def step(state, kv_n):
        return state * chunk_decay + kv_n, state

    init = jnp.zeros((B_, H, dk, dv), jnp.float32)
    _, prev_states = lax.scan(step, init, jnp.moveaxis(kv_chunk, 1, 0))
    prev_states = jnp.moveaxis(prev_states, 0, 1)
    o_cross = jnp.einsum('bnihd,hi,bnhde->bnihe', q, q_decay, prev_states)
    o = o_inner + o_cross
    o = o * lax.rsqrt(jnp.mean(o * o, axis=-1, keepdims=True) + RMS_EPS)
    o = o.reshape(B_, T, H * dv)
    y = jax.nn.silu(g.astype(jnp.float32)) * o
    return y.astype(h.dtype) @ w_o


def squared_relu_mlp(h, w_up, w_down):
    a = jax.nn.relu(h @ w_up)
    return (a * a) @ w_down


def setup_inputs(seed: int = 0) -> dict:
    key = jax.random.key(seed)
    ks = jax.random.split(key, 16)

    def dense(k, shape):
        return jax.random.normal(k, shape, jnp.float32) * (shape[-2] ** -0.5)

    def gain(k, shape):
        return 1.0 + 0.02 * jax.random.normal(k, shape, jnp.float32)

    swa_qkv_dim = (SWA_HEADS + 2 * SWA_KV_HEADS) * SWA_HEAD_DIM
    ret_in_dim = 2 * RET_HEADS * RET_QK_DIM + 2 * RET_HEADS * RET_V_DIM
    return {
        "x": jax.random.normal(ks[0], (BATCH, SEQ, D_MODEL), jnp.float32),
        "attn_norm": gain(ks[1], (DEPTH, D_MODEL)),
        "mlp_norm": gain(ks[2], (DEPTH, D_MODEL)),
        "final_norm": gain(ks[3], (D_MODEL,)),
        "swa_w_qkv": dense(ks[4], (N_A, D_MODEL, swa_qkv_dim)),
        "swa_sinks": 0.5 * jax.random.normal(ks[5], (N_A, SWA_HEADS), jnp.float32),
        "swa_w_o": dense(ks[6], (N_A, SWA_HEADS * SWA_HEAD_DIM, D_MODEL)),
        "sb_w_qkv": dense(ks[7], (N_B, D_MODEL, 3 * SB_HEADS * SB_HEAD_DIM)),
        "sb_w_o": dense(ks[8], (N_B, SB_HEADS * SB_HEAD_DIM, D_MODEL)),
        "ret_w_in": dense(ks[9], (N_C, D_MODEL, ret_in_dim)),
        "ret_w_o": dense(ks[10], (N_C, RET_HEADS * RET_V_DIM, D_MODEL)),
        "mlp_w_up": dense(ks[11], (DEPTH, D_MODEL, D_FF)),
        "mlp_w_down": dense(ks[12], (DEPTH, D_FF, D_MODEL)),
    }


def reference(x, attn_norm, mlp_norm, final_norm, swa_w_qkv, swa_sinks, swa_w_o,
              sb_w_qkv, sb_w_o, ret_w_in, ret_w_o, mlp_w_up, mlp_w_down):
    h = x
    for i in range(DEPTH):
        kind, j = i % N_MIXERS, i // N_MIXERS
        u = rms_norm(h, attn_norm[i])
        if kind == 0:
            mix = swa_mixer(u, swa_w_qkv[j], swa_sinks[j], swa_w_o[j])
        elif kind == 1:
            mix = stick_breaking_mixer(u, sb_w_qkv[j], sb_w_o[j])
        else:
            mix = retention_mixer(u, ret_w_in[j], ret_w_o[j])
        h = h + mix
        h = h + squared_relu_mlp(rms_norm(h, mlp_norm[i]), mlp_w_up[i], mlp_w_down[i])
    return rms_norm(h, final_norm)
```

```python
import numpy as np
from contextlib import ExitStack
import concourse.bass as bass
import concourse.mybir as mybir
from concourse.bass_utils import run_bass_kernel_spmd

F32 = mybir.dt.float32
BF16 = mybir.dt.bfloat16
AF = mybir.ActivationFunctionType
ALU = mybir.AluOpType
AX = mybir.AxisListType

D = 1024
KC = 8
T = 2048
TT = 512
NT = 4
NB = 16
EPS = 1e-6
ENGS = ['pe', 'act', 'dve', 'pool', 'sp']
GEN_MAX = 4000
NDS = 8
SB_LO = 16512
SB_HI = 229344


class Tr:
    __slots__ = ('w', 'r', 'x')

    def __init__(self, x=False):
        self.w = None
        self.r = {}
        self.x = x


class K:
    def __init__(self, nc, stack):
        self.nc = nc
        self.stack = stack
        self.ops = {e: [] for e in ENGS}
        self.semh = {}
        self.cnt = {}
        self.gen = {e: 0 for e in ENGS}
        self.waited = {e: {} for e in ENGS}
        self.dma_rr = {q: 0 for q in ENGS}
        self.dgen = {}
        self.ap = SB_LO
        self.nalloc = 0
        self.out_evs = []

    def alloc(self, shape, dt, name="t"):
        size = int(np.prod(shape[1:])) * (4 if dt == F32 else 2)
        size = (size + 63) // 64 * 64
        self.nalloc += 1
        t = self.nc.alloc_sbuf_tensor_at(f"{name}{self.nalloc}", list(shape), dt, offset=self.ap)
        self.ap += size
        assert self.ap <= SB_HI, f"SBUF overflow at {name}: {self.ap}"
        return t

    def mark(self):
        return self.ap

    def release(self, m):
        self.barrier()
        self.ap = m

    def sem(self, key):
        if key not in self.semh:
            nm = "s" + "_".join(str(x) for x in key)
            self.semh[key] = self.stack.enter_context(self.nc.semaphore(nm))
            self.cnt[key] = 0
        return self.semh[key]

    def _need(self, eng, evs):
        w = self.waited[eng]
        best = {}
        for ev in evs:
            if ev is None:
                continue
            k, v = ev
            if k[0] == eng:
                if eng == 'pe':
                    continue
                if v > self.cnt[k]:
                    continue
            if w.get(k, 0) >= v:
                continue
            if best.get(k, 0) < v:
                best[k] = v
        for k, v in best.items():
            w[k] = v
            self.ops[eng].append(('w', k, v))

    def _deps(self, reads, writes):
        evs = []
        for t in reads:
            evs.append(t.w)
            if t.x:
                evs.extend(t.r.values())
        for t in writes:
            evs.append(t.w)
            evs.extend(t.r.values())
        return evs

    def _mark(self, eng, ev, reads, writes):
        for t in writes:
            t.w = ev
            t.r = {}
        for t in reads:
            t.r[eng] = ev

    def op(self, eng, fn, reads=(), writes=(), inc=True):
        self._need(eng, self._deps(reads, writes))
        key = (eng, self.gen[eng])
        self.sem(key)
        if inc:
            self.cnt[key] += 1
            ev = (key, self.cnt[key])
            self.ops[eng].append(('o', fn, key, 1))
            if self.cnt[key] >= GEN_MAX:
                self.gen[eng] += 1
        else:
            ev = (key, self.cnt[key] + 1)
            self.ops[eng].append(('o', fn, None, 0))
        self._mark(eng, ev, reads, writes)
        return ev

    def dma(self, q, out_ap, in_ap, reads=(), writes=()):
        i = self.dma_rr[q]
        self.dma_rr[q] = (i + 1) % NDS
        g = self.dgen.get((q, i), 0)
        key = ('d' + q, i, g)
        self.sem(key)
        evs = self._deps(reads, writes)
        if self.cnt[key] > 0:
            evs.append((key, self.cnt[key]))
        elif g > 0:
            pk = ('d' + q, i, g - 1)
            evs.append((pk, self.cnt[pk]))
        self._need(q, evs)
        self.cnt[key] += 16
        ev = (key, self.cnt[key])
        self.ops[q].append(('o', lambda e: e.dma_start(out=out_ap, in_=in_ap), key, 16))
        if self.cnt[key] >= GEN_MAX:
            self.dgen[(q, i)] = g + 1
        self._mark(q, ev, reads, writes)
        return ev

    def all_events(self):
        evs = []
        for key, c in self.cnt.items():
            if c > 0:
                evs.append((key, c))
        return evs

    def barrier(self):
        evs = self.all_events()
        for e in ENGS:
            self._need(e, evs)

    def emit(self):
        nc = self.nc
        self._need('sp', self.all_events())
        ops = self.ops
        semh = self.semh

        def run(e, lst):
            for it in lst:
                if it[0] == 'w':
                    e.wait_ge(semh[it[1]], it[2])
                else:
                    ins = it[1](e)
                    if it[2] is not None:
                        ins.then_inc(semh[it[2]], it[3])

        with nc.Block() as block:
            @block.tensor
            def _(e):
                run(e, ops['pe'])

            @block.scalar
            def _(e):
                run(e, ops['act'])

            @block.vector
            def _(e):
                run(e, ops['dve'])

            @block.gpsimd
            def _(e):
                run(e, ops['pool'])

            @block.sync
            def _(e):
                run(e, ops['sp'])


class Rot:
    def __init__(self, k, shape, dt, n, name):
        self.tiles = [(k.alloc(shape, dt, name), Tr()) for _ in range(n)]
        self.i = 0

    def next(self):
        t = self.tiles[self.i]
        self.i = (self.i + 1) % len(self.tiles)
        return t


class Env:
    pass


def setup_common(k, nc, env, cm_d, gains_d):
    env.hT = k.alloc([128, KC, T], F32, "hT")
    env.hT_tr = [[Tr() for _ in range(NT)] for _ in range(KC)]
    env.uT = k.alloc([128, KC, T], BF16, "uT")
    env.uT_tr = [[Tr() for _ in range(NT)] for _ in range(KC)]
    env.cm = k.alloc([128, 4, 128], BF16, "cm")
    env.cm_tr = Tr()
    env.gains = k.alloc([128, 72], F32, "gains")
    env.gains_tr = Tr()
    env.banks = [(nc.alloc_psum_tensor(f"bank{i}", [128, 512], F32), Tr(True)) for i in range(8)]
    env.bi = 0
    k.dma('pool', env.cm[:], cm_d.ap(), writes=[env.cm_tr])
    k.dma('sp', env.gains[:], gains_d.ap(), writes=[env.gains_tr])
    env.epsc = k.alloc([128, 2], F32, "epsc")
    env.eps_tr = Tr()
    k.op('dve', lambda e: e.memset(env.epsc[:], EPS), writes=[env.eps_tr])
    env.ident = env.cm[:, 0, :]
    env.onesm = env.cm[:, 1, :]
    env.ones = env.cm[:, 2, :]
    env.tri = env.cm[:, 3, :]


def bank(env):
    b = env.banks[env.bi]
    env.bi = (env.bi + 1) % 8
    return b


def load_hT(k, env, src_d):
    v = src_d.ap().rearrange("(c p) t -> p c t", p=128)
    for c in range(KC):
        k.dma('sp', env.hT[:, c, :], v[:, c, :], writes=[env.hT_tr[c][tt] for tt in range(NT)])


def store_T(k, env, src, src_tr, dst_d):
    v = dst_d.ap().rearrange("(c p) t -> p c t", p=128)
    for c in range(KC):
        k.dma('sp', v[:, c, :], src[:, c, :], reads=[src_tr[c][tt] for tt in range(NT)])


def rmsnorm_tile(k, env, src, src_trs, gi, dst, dst_trs, n, sqp, rsp, dst_dt_is_bf=True):
    bk, bt = bank(env)
    for c in range(KC):
        sq, sqt = sqp.next()
        s_ap = src(c)
        k.op('act', lambda e, o=sq[:, 0:n], i=s_ap: e.activation(out=o, in_=i, func=AF.Square),
             reads=[src_trs[c]], writes=[sqt])
        k.op('pe', lambda e, o=bk[:, 0:n], r=sq[:, 0:n], st=(c == 0), sp=(c == KC - 1):
             e.matmul(o, env.onesm, r, start=st, stop=sp),
             reads=[sqt, env.cm_tr], writes=[bt])
    rs, rst = rsp.next()
    k.op('act', lambda e, o=rs[:, 0:n], i=bk[:, 0:n]: e.activation(out=o, in_=i, func=AF.Sqrt, bias=env.epsc[:, 0:1]),
         reads=[bt, env.eps_tr], writes=[rst])
    k.op('dve', lambda e, o=rs[:, 0:n]: e.reciprocal(out=o, in_=o), reads=[rst], writes=[rst])
    for c in range(KC):
        g_ap = env.gains[:, gi * 8 + c:gi * 8 + c + 1]
        k.op('dve', lambda e, o=dst(c), i=src(c), g=g_ap, r=rs[:, 0:n]:
             e.scalar_tensor_tensor(out=o, in0=i, scalar=g, in1=r, op0=ALU.mult, op1=ALU.mult),
             reads=[src_trs[c], rst, env.gains_tr], writes=[dst_trs[c]])


def norm_local(k, env, gi, sqp, rsp, dst=None, dst_tr=None):
    dst = env.uT if dst is None else dst
    dst_tr = env.uT_tr if dst_tr is None else dst_tr
    for tt in range(NT):
        sl = slice(tt * TT, (tt + 1) * TT)
        rmsnorm_tile(k, env,
                     lambda c, sl=sl: env.hT[:, c, sl], [env.hT_tr[c][tt] for c in range(KC)], gi,
                     lambda c, sl=sl: dst[:, c, sl], [dst_tr[c][tt] for c in range(KC)],
                     TT, sqp, rsp)


def mlp(k, env, li, wup_d, wdn_d):
    m = k.mark()
    sqp = Rot(k, [128, TT], BF16, 8, "sq")
    rsp = Rot(k, [128, TT], F32, 2, "rs")
    norm_local(k, env, 4 + li, sqp, rsp)
    wup = Rot(k, [128, KC, 512], BF16, 2, "wu")
    wdn = Rot(k, [128, 4, D], BF16, 3, "wd")
    rp = Rot(k, [128, TT], BF16, 4, "rl")
    ap_ = Rot(k, [128, 4, TT], BF16, 3, "aT")
    wu_v = wup_d.ap().rearrange("(kc p) f -> p kc f", p=128)
    NF = 8
    steps = [(ft, tt) for ft in range(NF) for tt in range(NT)]
    wts = {}

    def load_w(ft):
        wu, wut = wup.next()
        wd, wdt = wdn.next()
        k.dma('pool', wu[:], wu_v[:, :, ft * 512:(ft + 1) * 512], writes=[wut])
        wd_v = wdn_d.ap()[ft * 512:(ft + 1) * 512, :].rearrange("(fc p) d -> p fc d", p=128)
        k.dma('pool', wd[:], wd_v, writes=[wdt])
        wts[ft] = (wu, wut, wd, wdt)

    def up(s):
        ft, tt = steps[s]
        if tt == 0 and ft + 1 < NF:
            load_w(ft + 1)
        wu, wut, wd, wdt = wts[ft]
        a, at = ap_.next()
        sl = slice(tt * TT, (tt + 1) * TT)
        for fc in range(4):
            bk, bt = bank(env)
            for kc in range(KC):
                k.op('pe', lambda e, o=bk[:], l=wu[:, kc, fc * 128:(fc + 1) * 128], r=env.uT[:, kc, sl],
                     st=(kc == 0), sp=(kc == KC - 1): e.matmul(o, l, r, start=st, stop=sp),
                     reads=[wut, env.uT_tr[kc][tt]], writes=[bt], inc=(kc == KC - 1))
            r_, rt = rp.next()
            k.op('act', lambda e, o=r_[:], i=bk[:]: e.activation(out=o, in_=i, func=AF.Relu),
                 reads=[bt], writes=[rt])
            k.op('dve', lambda e, o=a[:, fc, :], i=r_[:]: e.tensor_tensor(out=o, in0=i, in1=i, op=ALU.mult),
                 reads=[rt], writes=[at])
        return a, at

    def down(s, a, at):
        ft, tt = steps[s]
        wu, wut, wd, wdt = wts[ft]
        sl = slice(tt * TT, (tt + 1) * TT)
        for dc in range(KC):
            bk, bt = bank(env)
            for fc in range(4):
                k.op('pe', lambda e, o=bk[:], l=wd[:, fc, dc * 128:(dc + 1) * 128], r=a[:, fc, :],
                     st=(fc == 0), sp=(fc == 3): e.matmul(o, l, r, start=st, stop=sp),
                     reads=[wdt, at], writes=[bt], inc=(fc == 3))
            k.op('dve', lambda e, o=env.hT[:, dc, sl], i=bk[:]: e.tensor_tensor(out=o, in0=o, in1=i, op=ALU.add),
                 reads=[bt, env.hT_tr[dc][tt]], writes=[env.hT_tr[dc][tt]])

    load_w(0)
    prev = up(0)
    for s in range(len(steps)):
        nxt = up(s + 1) if s + 1 < len(steps) else None
        down(s, *prev)
        prev = nxt
    k.release(m)


def final_norm(k, env, out_d):
    m = k.mark()
    sqp = Rot(k, [128, TT], BF16, 8, "sq")
    rsp = Rot(k, [128, TT], F32, 2, "rs")
    ob = Rot(k, [128, KC, TT], F32, 2, "ob")
    v = out_d.ap().rearrange("(c p) t -> p c t", p=128)
    for tt in range(NT):
        sl = slice(tt * TT, (tt + 1) * TT)
        o, ot = ob.next()
        rmsnorm_tile(k, env,
                     lambda c, sl=sl: env.hT[:, c, sl], [env.hT_tr[c][tt] for c in range(KC)], 8,
                     lambda c, o=o: o[:, c, :], [ot] * KC, TT, sqp, rsp)
        ev = k.dma('sp', v[:, :, sl], o[:], reads=[ot])
    k.release(m)


def make_cm():
    cm = np.zeros((128, 4, 128), np.float32)
    cm[:, 0, :] = np.eye(128, dtype=np.float32)
    cm[:, 1, :] = 1.0 / 1024.0
    cm[:, 2, :] = 1.0
    j = np.arange(128)[:, None]
    s = np.arange(128)[None, :]
    cm[:, 3, :] = (j >= s).astype(np.float32)
    return cm


def make_gains(attn_norm, mlp_norm, final_norm):
    g = np.concatenate([attn_norm, mlp_norm, final_norm[None, :]], axis=0)
    return np.ascontiguousarray(g.reshape(9, 8, 128).transpose(2, 0, 1).reshape(128, 72))


def build_test_mlp():
    nc = bass.Bass("TRN2", target_bir_lowering=False)
    stack = ExitStack()
    k = K(nc, stack)
    env = Env()
    hin = nc.dram_tensor("hin", [D, T], F32, kind="ExternalInput")
    cm_d = nc.dram_tensor("cm", [128, 4, 128], F32, kind="ExternalInput")
    g_d = nc.dram_tensor("gains", [128, 72], F32, kind="ExternalInput")
    wup = nc.dram_tensor("wup", [D, 4096], F32, kind="ExternalInput")
    wdn = nc.dram_tensor("wdn", [4096, D], F32, kind="ExternalInput")
    hout = nc.dram_tensor("hout", [D, T], F32, kind="ExternalOutput")
    setup_common(k, nc, env, cm_d, g_d)
    load_hT(k, env, hin)
    mlp(k, env, 0, wup, wdn)
    store_T(k, env, env.hT, env.hT_tr, hout)
    k.emit()
    stack.close()
    return nc


class Banks:
    def __init__(self, env, idxs):
        self.b = [env.banks[i] for i in idxs]
        self.i = 0

    def next(self):
        t = self.b[self.i]
        self.i = (self.i + 1) % len(self.b)
        return t


def evac(k, eng, out_ap, in_ap, reads, writes, scale=None):
    if eng == 'act':
        if scale is None:
            k.op('act', lambda e: e.activation(out=out_ap, in_=in_ap, func=AF.Copy), reads=reads, writes=writes)
        else:
            k.op('act', lambda e: e.activation(out=out_ap, in_=in_ap, func=AF.Copy, scale=scale),
                 reads=reads, writes=writes)
    else:
        if scale is None:
            k.op('dve', lambda e: e.tensor_copy(out=out_ap, in_=in_ap), reads=reads, writes=writes)
        else:
            k.op('dve', lambda e: e.tensor_scalar(out=out_ap, in0=in_ap, scalar1=scale, scalar2=None, op0=ALU.mult),
                 reads=reads, writes=writes)


def swa_layer(k, env, li, wqkv, sinks2_d, wo_d, halo_d, E_d, Ef_d, dbg=99):
    m = k.mark()
    wq = k.alloc([128, KC, 1024], BF16, "wq")
    wq_tr = Tr()
    kT2 = k.alloc([128, 2, 17 * 128], BF16, "kT2")
    kT2_tr = [[Tr() for _ in range(NT + 1)] for _ in range(2)]
    vtok = k.alloc([128, 17, 2, 192], BF16, "vtok")
    vtok_tr = [Tr() for _ in range(17)]
    qTp = Rot(k, [128, KC, TT], BF16, 2, "qT")
    wv_ = wqkv.rearrange("(kc p) n -> p kc n", p=128)
    k.dma('pool', wq[:], wv_[:, :, 0:1024], writes=[wq_tr])
    mA = k.mark()
    sqp = Rot(k, [128, TT], BF16, 8, "sq")
    rsp = Rot(k, [128, TT], F32, 2, "rs")
    norm_local(k, env, li, sqp, rsp)
    hh = k.alloc([128, KC, 128], F32, "hh")
    hh_tr = Tr()
    uh = k.alloc([128, KC, 128], BF16, "uh")
    uh_tr = Tr()
    k.dma('sp', hh[:], halo_d.rearrange("(c p) t -> p c t", p=128), writes=[hh_tr])
    rmsnorm_tile(k, env, lambda c: hh[:, c, :], [hh_tr] * KC, li, lambda c: uh[:, c, :], [uh_tr] * KC,
                 128, sqp, rsp)
    wk2 = k.alloc([128, KC, 256], BF16, "wk2")
    wk2_tr = Tr()
    for kvh in range(2):
        for rep in range(2):
            o = (kvh * 2 + rep) * 64
            k.dma('pool', wk2[:, :, o:o + 64], wv_[:, :, 1024 + kvh * 64:1024 + (kvh + 1) * 64], writes=[wk2_tr])
    wvv = k.alloc([128, KC, 128], BF16, "wv")
    wvv_tr = Tr()
    k.dma('pool', wvv[:], wv_[:, :, 1152:1280], writes=[wvv_tr])
    k.op('dve', lambda e: e.memset(vtok[:], 0.0), writes=vtok_tr)
    ei = 0
    for kvh in range(2):
        for tt in range(NT + 1):
            bk, bt = bank(env)
            n = 128 if tt == 0 else TT
            for kc in range(KC):
                if tt == 0:
                    r_ap, r_tr = uh[:, kc, :], uh_tr
                else:
                    r_ap, r_tr = env.uT[:, kc, (tt - 1) * TT:tt * TT], env.uT_tr[kc][tt - 1]
                k.op('pe', lambda e, o=bk[:, 0:n], l=wk2[:, kc, kvh * 128:(kvh + 1) * 128], r=r_ap,
                     st=(kc == 0), sp=(kc == KC - 1): e.matmul(o, l, r, start=st, stop=sp),
                     reads=[wk2_tr, r_tr], writes=[bt], inc=(kc == KC - 1))
            o0 = 0 if tt == 0 else 128 + (tt - 1) * TT
            evac(k, ('act', 'dve')[ei % 2], kT2[:, kvh, o0:o0 + n], bk[:, 0:n], [bt], [kT2_tr[kvh][tt]])
            ei += 1
    for blk in range(17):
        bk, bt = bank(env)
        for kc in range(KC):
            if blk == 0:
                l_ap, l_tr = uh[:, kc, :], uh_tr
            else:
                l_ap, l_tr = env.uT[:, kc, (blk - 1) * 128:blk * 128], env.uT_tr[kc][(blk - 1) // 4]
            k.op('pe', lambda e, o=bk[:, 0:128], l=l_ap, r=wvv[:, kc, :], st=(kc == 0), sp=(kc == KC - 1):
                 e.matmul(o, l, r, start=st, stop=sp),
                 reads=[wvv_tr, l_tr], writes=[bt], inc=(kc == KC - 1))
        evac(k, ('act', 'dve')[ei % 2], vtok[:, blk, :, 64:128],
             bk[:, 0:128].rearrange("p (a d) -> p a d", a=2), [bt], [vtok_tr[blk]])
        ei += 1
    k.release(mA)
    if dbg <= 1:
        k.release(m)
        return
    wo = k.alloc([128, KC, 1024], BF16, "wo")
    wo_tr = Tr()
    wo_v = wo_d.rearrange("(c p) n -> p c n", p=128)
    for hq in range(2):
        k.dma('pool', wo[:, hq * 4:(hq + 1) * 4, :], wo_v[:, hq * 4:(hq + 1) * 4, :], writes=[wo_tr])
    E = k.alloc([128, 2, 4, 512], BF16, "E")
    E_tr = Tr()
    k.dma('pool', E[:], E_d.ap(), writes=[E_tr])
    Ef = k.alloc([128, 4, 512], BF16, "Ef")
    Ef_tr = Tr()
    k.dma('pool', Ef[:], Ef_d.ap(), writes=[Ef_tr])
    esk = k.alloc([128, 8], F32, "esk")
    esk_tr = Tr()
    k.dma('sp', esk[:], sinks2_d.ap(), writes=[esk_tr])
    k.op('act', lambda e: e.activation(out=esk[:], in_=esk[:], func=AF.Exp), reads=[esk_tr], writes=[esk_tr])
    zt = k.alloc([128, 128], F32, "zt")
    zt_tr = Tr()
    k.op('dve', lambda e: e.memset(zt[:], 0.0), writes=[zt_tr])
    esf = k.alloc([128, 8, 128], F32, "esf")
    esf_tr = Tr()
    for c in range(8):
        k.op('dve', lambda e, c=c: e.tensor_scalar(out=esf[:, c, :], in0=zt[:], scalar1=esk[:, c:c + 1], scalar2=None,
                                                   op0=ALU.add),
             reads=[zt_tr, esk_tr], writes=[esf_tr])
    ones2 = k.alloc([128, 192], BF16, "ones2")
    ones2_tr = Tr()
    k.op('dve', lambda e: e.memset(ones2[:], 0.0), writes=[ones2_tr])
    k.op('dve', lambda e: e.memset(ones2[:, 64:128], 1.0), reads=[ones2_tr], writes=[ones2_tr])
    if dbg <= 2:
        k.release(m)
        return
    stp = Banks(env, [0, 1, 2, 3])
    pop = Banks(env, [4, 5, 6, 7])
    p0p = Rot(k, [128, 512], F32, 3, "p0")
    pTp = Rot(k, [128, 1024], BF16, 4, "pT")
    tp = Rot(k, [128, 512], F32, 2, "tden")
    for tt in range(NT):
        sl = slice(tt * TT, (tt + 1) * TT)
        qT, qT_t = qTp.next()
        for c in range(KC):
            bk, bt = stp.next()
            for kc in range(KC):
                k.op('pe', lambda e, o=bk[:], l=wq[:, kc, c * 128:(c + 1) * 128], r=env.uT[:, kc, sl],
                     st=(kc == 0), sp=(kc == KC - 1): e.matmul(o, l, r, start=st, stop=sp),
                     reads=[wq_tr, env.uT_tr[kc][tt]], writes=[bt], inc=(kc == KC - 1))
            evac(k, ('act', 'dve')[c % 2], qT[:, c, :], bk[:], [bt], [qT_t])
        for bl in range(4 if dbg > 3 else 0):
            blk = tt * 4 + bl
            qs = slice(bl * 128, (bl + 1) * 128)
            for kvh in range(2):
                pTs = []
                for j in range(2):
                    kb = blk + j
                    kt_tr = kT2_tr[kvh][0 if kb == 0 else 1 + (kb - 1) // 4]
                    sts = [stp.next(), stp.next()]
                    for g in range(8):
                        half = g // 4
                        c = kvh * 4 + g % 4
                        bk, bt = sts[half]
                        col = (g % 4) * 128
                        ps_ = slice(half * 64, (half + 1) * 64)
                        k.op('pe', lambda e, o=bk[:, col:col + 128], l=kT2[ps_, kvh, kb * 128:(kb + 1) * 128],
                             r=qT[ps_, c, qs]: e.matmul(o, l, r, start=True, stop=True),
                             reads=[kt_tr, qT_t], writes=[bt], inc=(g % 4 == 3))
                    pT, pTt = pTp.next()
                    for b in range(2):
                        bk, bt = sts[b]
                        p0, p0t = p0p.next()
                        k.op('act', lambda e, o=p0[:], i=bk[:]:
                             e.activation(out=o, in_=i, func=AF.Exp, scale=0.125),
                             reads=[bt], writes=[p0t])
                        if blk == 0 and j == 0:
                            e_ap = Ef[:, kvh * 2 + b, :]
                            e_tr = Ef_tr
                        else:
                            e_ap = E[:, j, kvh * 2 + b, :]
                            e_tr = E_tr
                        k.op('dve', lambda e, o=pT[:, b * 512:(b + 1) * 512], a=p0[:], b_=e_ap:
                             e.tensor_tensor(out=o, in0=a, in1=b_, op=ALU.mult),
                             reads=[p0t, e_tr], writes=[pTt])
                    pTs.append((pT, pTt, kb))
                if dbg <= 4:
                    continue
                po, pot = pop.next()
                pd, pdt = pop.next()
                for cc in range(4):
                    osl = slice(cc * 128, (cc + 1) * 128)
                    n_ = 0
                    for half in range(2):
                        g = half * 4 + cc
                        vs = slice(64, 192) if half == 0 else slice(0, 128)
                        for j in range(2):
                            pT, pTt, kb = pTs[j]
                            k.op('pe', lambda e, o=po[:, osl], l=vtok[:, kb, kvh, vs], r=pT[:, g * 128:(g + 1) * 128],
                                 st=(n_ == 0), sp=(n_ == 3): e.matmul(o, l, r, start=st, stop=sp),
                                 reads=[vtok_tr[kb], pTt], writes=[pot], inc=False)
                            k.op('pe', lambda e, o=pd[:, osl], l=ones2[:, vs], r=pT[:, g * 128:(g + 1) * 128],
                                 st=(n_ == 0), sp=(n_ == 3): e.matmul(o, l, r, start=st, stop=sp),
                                 reads=[ones2_tr, pTt], writes=[pdt], inc=(cc == 3 and n_ == 3))
                            n_ += 1
                if dbg <= 5:
                    continue
                t_, tt_ = tp.next()
                k.op('dve', lambda e, o=t_[:], a=pd[:],
                     b_=esf[:, kvh * 4:(kvh + 1) * 4, :].rearrange("p c q -> p (c q)"):
                     e.tensor_tensor(out=o, in0=a, in1=b_, op=ALU.add),
                     reads=[pdt, esf_tr], writes=[tt_])
                k.op('dve', lambda e, o=t_[:]: e.reciprocal(out=o, in_=o), reads=[tt_], writes=[tt_])
                gq = slice(blk * 128, (blk + 1) * 128)
                k.op('dve', lambda e, o=env.uT[:, kvh * 4:(kvh + 1) * 4, gq],
                     a=po[:].rearrange("p (c q) -> p c q", c=4),
                     b_=t_[:].rearrange("p (c q) -> p c q", c=4):
                     e.tensor_tensor(out=o, in0=a, in1=b_, op=ALU.mult),
                     reads=[pot, tt_], writes=[env.uT_tr[kvh * 4 + cc][tt] for cc in range(4)])
        for dc in range(KC if dbg > 6 else 0):
            bk, bt = stp.next()
            for c in range(KC):
                k.op('pe', lambda e, o=bk[:], l=wo[:, c, dc * 128:(dc + 1) * 128], r=env.uT[:, c, sl],
                     st=(c == 0), sp=(c == KC - 1): e.matmul(o, l, r, start=st, stop=sp),
                     reads=[wo_tr, env.uT_tr[c][tt]], writes=[bt], inc=(c == KC - 1))
            k.op('dve', lambda e, o=env.hT[:, dc, sl], i=bk[:]: e.tensor_tensor(out=o, in0=o, in1=i, op=ALU.add),
                 reads=[bt, env.hT_tr[dc][tt]], writes=[env.hT_tr[dc][tt]])
    k.release(m)


def make_swa_tables(first_half):
    slopes = np.exp2(-8.0 * np.arange(1, 17, dtype=np.float64) / 16.0)
    s = np.arange(128)[:, None, None, None]
    j = np.arange(2)[None, :, None, None]
    q = np.arange(128)[None, None, None, :]
    dist = 128 + q - (j * 128 + s)
    mask = (dist >= 0) & (dist < 128)
    E = np.exp(-slopes[None, None, :, None] * dist) * mask
    E = E.astype(np.float32)
    E = E.reshape(128, 2, 2, 4, 2, 128).transpose(0, 1, 2, 4, 3, 5).reshape(128, 2, 4, 512)
    E = np.ascontiguousarray(E)
    Ef = np.zeros((128, 4, 512), np.float32) if first_half else np.ascontiguousarray(E[:, 0])
    return E, Ef


def make_sinks2(sinks):
    return np.ascontiguousarray(sinks.reshape(8, 2).T[np.arange(128) // 64, :]).astype(np.float32)


def build_layer(li, final=False, with_mlp=True, dbg=99):
    nc = bass.Bass("TRN2", target_bir_lowering=False)
    stack = ExitStack()
    k = K(nc, stack)
    env = Env()
    kind = li % 3
    hin = nc.dram_tensor("hin", [D, T], F32, kind="ExternalInput")
    cm_d = nc.dram_tensor("cm", [128, 4, 128], F32, kind="ExternalInput")
    g_d = nc.dram_tensor("gains", [128, 72], F32, kind="ExternalInput")
    wup = nc.dram_tensor("wup", [D, 4096], F32, kind="ExternalInput")
    wdn = nc.dram_tensor("wdn", [4096, D], F32, kind="ExternalInput")
    hout = nc.dram_tensor("hout", [D, T], F32, kind="ExternalOutput")
    if kind == 0:
        rem = nc.dram_tensor("rem", [D, 128], F32, kind="ExternalInput")
        wqkv = nc.dram_tensor("wqkv", [D, 1280], F32, kind="ExternalInput")
        wo = nc.dram_tensor("wo", [D, D], F32, kind="ExternalInput")
        sinks2 = nc.dram_tensor("sinks2", [128, 8], F32, kind="ExternalInput")
        E_d = nc.dram_tensor("swaE", [128, 2, 4, 512], F32, kind="ExternalInput")
        Ef_d = nc.dram_tensor("swaEf", [128, 4, 512], F32, kind="ExternalInput")
    setup_common(k, nc, env, cm_d, g_d)
    load_hT(k, env, hin)
    if kind == 2:
        rem = nc.dram_tensor("rem", [D, T], F32, kind="ExternalInput")
        win = nc.dram_tensor("win", [D, 6144], F32, kind="ExternalInput")
        wo = nc.dram_tensor("wo", [2048, D], F32, kind="ExternalInput")
        decT_d = nc.dram_tensor("retD", [128, 512], F32, kind="ExternalInput")
        qdec_d = nc.dram_tensor("retQ", [128, 8, 128], F32, kind="ExternalInput")
        kdec_d = nc.dram_tensor("retK", [128, 1024], F32, kind="ExternalInput")
        ret_layer(k, env, nc, li, win.ap(), wo.ap(), rem.ap(), decT_d, qdec_d, kdec_d)
    if kind == 1:
        rem = nc.dram_tensor("rem", [D, T], F32, kind="ExternalInput")
        wqkv = nc.dram_tensor("wqkv", [D, 3072], F32, kind="ExternalInput")
        wo = nc.dram_tensor("wo", [D, D], F32, kind="ExternalInput")
        M_d = nc.dram_tensor("sbM", [128, 4, 512], F32, kind="ExternalInput")
        sb_layer(k, env, nc, li, wqkv.ap(), wo.ap(), rem.ap(), M_d)
    if kind == 0:
        swa_layer(k, env, li, wqkv.ap(), sinks2, wo.ap(), rem.ap(), E_d, Ef_d, dbg=dbg)
    if with_mlp:
        mlp(k, env, li, wup, wdn)
    if final:
        final_norm(k, env, hout)
    else:
        store_T(k, env, env.hT, env.hT_tr, hout)
    k.emit()
    stack.close()
    return nc


def norm_remote(k, env, gi, rem_d, uT_rem, uT_rem_tr, flagged=False):
    m = k.mark()
    sqp = Rot(k, [128, TT], BF16, 8, "sq")
    rsp = Rot(k, [128, TT], F32, 2, "rs")
    hrp = Rot(k, [128, KC, TT], F32, 2, "hr")
    v = rem_d.rearrange("(c p) t -> p c t", p=128)
    for tt in range(NT):
        sl = slice(tt * TT, (tt + 1) * TT)
        hr, hrt = hrp.next()
        k.dma('sp', hr[:], v[:, :, sl], writes=[hrt])
        rmsnorm_tile(k, env, lambda c, hr=hr: hr[:, c, :], [hrt] * KC, gi,
                     lambda c, sl=sl: uT_rem[:, c, sl], [uT_rem_tr[c][tt] for c in range(KC)], TT, sqp, rsp)
    k.release(m)


def sb_layer(k, env, nc, li, wqkv, wo_d, rem_d, M_d):
    qT_s = nc.dram_tensor(f"sb_qT_s{li}", [8, 128, T], BF16)
    kT_s = nc.dram_tensor(f"sb_kT_s{li}", [8, 128, 2 * T], BF16)
    v_s = nc.dram_tensor(f"sb_v_s{li}", [32, 128, D], BF16)
    qT_s_tr = [[Tr() for _ in range(NT)] for _ in range(8)]
    kT_s_tr = [[Tr() for _ in range(2 * NT)] for _ in range(8)]
    v_s_tr = [Tr() for _ in range(32)]
    m = k.mark()
    uT_rem = k.alloc([128, KC, T], BF16, "uTrem")
    uT_rem_tr = [[Tr() for _ in range(NT)] for _ in range(KC)]
    m1 = k.mark()
    sqp = Rot(k, [128, TT], BF16, 8, "sq")
    rsp = Rot(k, [128, TT], F32, 2, "rs")
    norm_local(k, env, li, sqp, rsp)
    k.release(m1)
    norm_remote(k, env, li, rem_d, uT_rem, uT_rem_tr)
    m1 = k.mark()
    wgp = Rot(k, [128, KC, 1024], BF16, 2, "wg")
    stg = Rot(k, [128, TT], BF16, 4, "stg")
    vstg = Rot(k, [128, D], BF16, 3, "vstg")
    w_v = wqkv.rearrange("(kc p) n -> p kc n", p=128)
    ei = 0
    wg, wgt = wgp.next()
    k.dma('pool', wg[:], w_v[:, :, 0:1024], writes=[wgt])
    wk_, wkt = wgp.next()
    k.dma('pool', wk_[:], w_v[:, :, 1024:2048], writes=[wkt])
    for c in range(8):
        for tt in range(NT):
            bk, bt = bank(env)
            sl = slice(tt * TT, (tt + 1) * TT)
            for kc in range(KC):
                k.op('pe', lambda e, o=bk[:], l=wg[:, kc, c * 128:(c + 1) * 128], r=env.uT[:, kc, sl],
                     st=(kc == 0), sp=(kc == KC - 1): e.matmul(o, l, r, start=st, stop=sp),
                     reads=[wgt, env.uT_tr[kc][tt]], writes=[bt], inc=(kc == KC - 1))
            s_, st_ = stg.next()
            evac(k, ('act', 'dve')[ei % 2], s_[:], bk[:], [bt], [st_])
            ei += 1
            k.dma('sp', qT_s.ap()[c, :, sl], s_[:], reads=[st_], writes=[qT_s_tr[c][tt]])
    for c in range(8):
        for t8 in range(2 * NT):
            bk, bt = bank(env)
            tt = t8 % NT
            sl = slice(tt * TT, (tt + 1) * TT)
            src, src_tr = (uT_rem, uT_rem_tr) if t8 < NT else (env.uT, env.uT_tr)
            for kc in range(KC):
                k.op('pe', lambda e, o=bk[:], l=wk_[:, kc, c * 128:(c + 1) * 128], r=src[:, kc, sl],
                     st=(kc == 0), sp=(kc == KC - 1): e.matmul(o, l, r, start=st, stop=sp),
                     reads=[wkt, src_tr[kc][tt]], writes=[bt], inc=(kc == KC - 1))
            s_, st_ = stg.next()
            evac(k, ('act', 'dve')[ei % 2], s_[:], bk[:], [bt], [st_])
            ei += 1
            k.dma('sp', kT_s.ap()[c, :, t8 * TT:(t8 + 1) * TT], s_[:], reads=[st_], writes=[kT_s_tr[c][t8]])
    wv_, wvt = wgp.next()
    k.dma('pool', wv_[:], w_v[:, :, 2048:3072], writes=[wvt])
    for blk in range(32):
        lb = blk % 16
        src, src_tr = (uT_rem, uT_rem_tr) if blk < 16 else (env.uT, env.uT_tr)
        vs_, vst_ = vstg.next()
        for hf in range(2):
            bk, bt = bank(env)
            for kc in range(KC):
                k.op('pe', lambda e, o=bk[:], l=src[:, kc, lb * 128:(lb + 1) * 128], r=wv_[:, kc, hf * 512:(hf + 1) * 512],
                     st=(kc == 0), sp=(kc == KC - 1): e.matmul(o, l, r, start=st, stop=sp),
                     reads=[wvt, src_tr[kc][lb // 4]], writes=[bt], inc=(kc == KC - 1))
            evac(k, ('act', 'dve')[ei % 2], vs_[:, hf * 512:(hf + 1) * 512], bk[:], [bt], [vst_])
            ei += 1
        k.dma('sp', v_s.ap()[blk], vs_[:], reads=[vst_], writes=[v_s_tr[blk]])
    k.release(m1)
    k.release(m)
    m = k.mark()
    wo = k.alloc([128, KC, 1024], BF16, "wo")
    wo_tr = Tr()
    wo_v = wo_d.rearrange("(c p) n -> p c n", p=128)
    for hq in range(2):
        k.dma('pool', wo[:, hq * 4:(hq + 1) * 4, :], wo_v[:, hq * 4:(hq + 1) * 4, :], writes=[wo_tr])
    Mk = k.alloc([128, 4, TT], BF16, "Mk")
    Mk_tr = Tr()
    k.dma('pool', Mk[:], M_d.ap(), writes=[Mk_tr])
    onec = k.alloc([128, 2], F32, "onec")
    onec_tr = Tr()
    k.op('dve', lambda e: e.memset(onec[:], 1.0), writes=[onec_tr])
    qcp = Rot(k, [128, T], BF16, 2, "qc")
    kcp = Rot(k, [128, 2 * T], BF16, 2, "kc")
    vcp = Rot(k, [128, 32, 128], BF16, 2, "vc")
    ep = Rot(k, [128, TT], F32, 3, "e")
    spp = Rot(k, [128, TT], BF16, 3, "sp")
    tmpp = Rot(k, [128, TT], F32, 2, "tmp")
    exp_ = Rot(k, [128, TT], F32, 2, "ex")
    Ap = Rot(k, [128, TT], BF16, 3, "A")
    Rp = Rot(k, [128, TT], F32, 2, "Racc")
    zbp = Banks(env, [0, 1])
    cbp = Banks(env, [2, 3])
    csp = Banks(env, [4, 5])
    pop = Banks(env, [6, 7])
    for c in range(8):
        qc, qct = qcp.next()
        kc_, kct = kcp.next()
        vc, vct = vcp.next()
        k.dma('sp', qc[:], qT_s.ap()[c], reads=qT_s_tr[c], writes=[qct])
        for hf in range(2):
            k.dma('sp', kc_[:, hf * T:(hf + 1) * T], kT_s.ap()[c, :, hf * T:(hf + 1) * T],
                  reads=kT_s_tr[c][hf * NT:(hf + 1) * NT], writes=[kct])
        for q4 in range(4):
            k.dma('sp', vc[:, q4 * 8:(q4 + 1) * 8, :],
                  v_s.ap()[q4 * 8:(q4 + 1) * 8, :, c * 128:(c + 1) * 128].rearrange("b p d -> p b d"),
                  reads=v_s_tr[q4 * 8:(q4 + 1) * 8], writes=[vct])
        for hh in range(2):
            ps_ = slice(hh * 64, (hh + 1) * 64)
            for qt in range(NT):
                gb0 = 16 + 4 * qt
                qsl = slice(qt * TT, (qt + 1) * TT)
                R, Rt = Rp.next()
                k.op('pool', lambda e, o=R[:]: e.memset(o, 0.0), writes=[Rt])
                po, pot = pop.next()
                top = gb0 + 3
                for kb in range(top, -1, -1):
                    diag = kb >= gb0
                    qi = kb - gb0
                    zb, zbt = zbp.next()
                    k.op('pe', lambda e, o=zb[:], l=kc_[ps_, kb * 128:(kb + 1) * 128], r=qc[ps_, qsl]:
                         e.matmul(o, l, r, start=True, stop=True),
                         reads=[kct, qct], writes=[zbt])
                    e_, et = ep.next()
                    k.op('act', lambda e, o=e_[:], i=zb[:]: e.activation(out=o, in_=i, func=AF.Exp, scale=0.125),
                         reads=[zbt], writes=[et])
                    sp_, spt = spp.next()
                    k.op('act', lambda e, o=sp_[:], i=e_[:]: e.activation(out=o, in_=i, func=AF.Ln, bias=onec[:, 0:1]),
                         reads=[et, onec_tr], writes=[spt])
                    if diag:
                        k.op('dve', lambda e, o=sp_[:], mk=Mk[:, qi, :]: e.tensor_tensor(out=o, in0=o, in1=mk, op=ALU.mult),
                             reads=[spt, Mk_tr], writes=[spt])
                    cb, cbt = cbp.next()
                    k.op('pe', lambda e, o=cb[:], r=sp_[:]: e.matmul(o, env.tri, r, start=True, stop=True),
                         reads=[spt, env.cm_tr], writes=[cbt])
                    cs, cst = csp.next()
                    k.op('pe', lambda e, o=cs[:], r=sp_[:]: e.matmul(o, env.ones, r, start=True, stop=True),
                         reads=[spt, env.cm_tr], writes=[cst])
                    tmp, tmpt = tmpp.next()
                    k.op('dve', lambda e, o=tmp[:], a=cb[:], b_=R[:]: e.tensor_tensor(out=o, in0=a, in1=b_, op=ALU.add),
                         reads=[cbt, Rt], writes=[tmpt])
                    ex, ext = exp_.next()
                    k.op('act', lambda e, o=ex[:], i=tmp[:]: e.activation(out=o, in_=i, func=AF.Exp, scale=-1.0),
                         reads=[tmpt], writes=[ext])
                    A, At = Ap.next()
                    if diag:
                        k.op('pool', lambda e, o=ex[:], a=e_[:]: e.tensor_tensor(out=o, in0=o, in1=a, op=ALU.mult),
                             reads=[ext, et], writes=[ext])
                        k.op('pool', lambda e, o=A[:], a=ex[:], mk=Mk[:, qi, :]:
                             e.tensor_tensor(out=o, in0=a, in1=mk, op=ALU.mult),
                             reads=[ext, Mk_tr], writes=[At])
                    else:
                        k.op('pool', lambda e, o=A[:], a=e_[:], b_=ex[:]: e.tensor_tensor(out=o, in0=a, in1=b_, op=ALU.mult),
                             reads=[et, ext], writes=[At])
                    if kb > 0:
                        R2, R2t = Rp.next()
                        k.op('dve', lambda e, o=R2[:], a=cs[:], b_=R[:]: e.tensor_tensor(out=o, in0=a, in1=b_, op=ALU.add),
                             reads=[cst, Rt], writes=[R2t])
                        R, Rt = R2, R2t
                    k.op('pe', lambda e, o=po[:], l=vc[:, kb, :], r=A[:], st=(kb == top), sp=(kb == 0):
                         e.matmul(o, l, r, start=st, stop=sp),
                         reads=[vct, At], writes=[pot])
                evac(k, ('act', 'dve')[(hh + qt) % 2], env.uT[ps_, c, qsl], po[ps_, :], [pot], [env.uT_tr[c][qt]])
    for tt in range(NT):
        sl = slice(tt * TT, (tt + 1) * TT)
        for dc in range(KC):
            bk, bt = bank(env)
            for c in range(KC):
                k.op('pe', lambda e, o=bk[:], l=wo[:, c, dc * 128:(dc + 1) * 128], r=env.uT[:, c, sl],
                     st=(c == 0), sp=(c == KC - 1): e.matmul(o, l, r, start=st, stop=sp),
                     reads=[wo_tr, env.uT_tr[c][tt]], writes=[bt], inc=(c == KC - 1))
            k.op('dve', lambda e, o=env.hT[:, dc, sl], i=bk[:]: e.tensor_tensor(out=o, in0=o, in1=i, op=ALU.add),
                 reads=[bt, env.hT_tr[dc][tt]], writes=[env.hT_tr[dc][tt]])
    k.release(m)


def make_sb_masks():
    s = np.arange(128)[:, None, None]
    qi = np.arange(4)[None, :, None]
    t = np.arange(512)[None, None, :]
    tb = t // 128
    M = np.where(tb > qi, 1.0, np.where(tb == qi, (s < (t % 128)).astype(np.float64), 0.0))
    return np.ascontiguousarray(M.astype(np.float32))


RET_GAMMA = [1.0 - 2.0 ** (-5 - h) for h in range(4)]


def ret_layer(k, env, nc, li, win, wo_d, rem_d, decT_d, qdec_d, kdec_d):
    qT_s = nc.dram_tensor(f"rt_qT_s{li}", [8, 128, T], BF16)
    kT_s = nc.dram_tensor(f"rt_kT_s{li}", [8, 128, T], BF16)
    kd_s = nc.dram_tensor(f"rt_kd_s{li}", [32, 128, 1024], BF16)
    v_s = nc.dram_tensor(f"rt_v_s{li}", [32, 128, 2048], BF16)
    sg_s = nc.dram_tensor(f"rt_sg_s{li}", [16, 128, 2048], BF16)
    qT_s_tr = [[Tr() for _ in range(NT)] for _ in range(8)]
    kT_s_tr = [[Tr() for _ in range(NT)] for _ in range(8)]
    kd_s_tr = [Tr() for _ in range(32)]
    v_s_tr = [Tr() for _ in range(32)]
    sg_s_tr = [Tr() for _ in range(16)]
    m = k.mark()
    uT_rem = k.alloc([128, KC, T], BF16, "uTrem")
    uT_rem_tr = [[Tr() for _ in range(NT)] for _ in range(KC)]
    m1 = k.mark()
    sqp = Rot(k, [128, TT], BF16, 8, "sq")
    rsp = Rot(k, [128, TT], F32, 2, "rs")
    norm_local(k, env, li, sqp, rsp)
    k.release(m1)
    norm_remote(k, env, li, rem_d, uT_rem, uT_rem_tr)
    m1 = k.mark()
    wgp = Rot(k, [128, KC, 1024], BF16, 2, "wg")
    stg = Rot(k, [128, TT], BF16, 4, "stg")
    vstg = Rot(k, [128, 1024], BF16, 3, "vstg")
    kdec = k.alloc([128, 1024], F32, "kdec")
    kdec_tr = Tr()
    k.dma('sp', kdec[:], kdec_d.ap(), writes=[kdec_tr])
    w_v = win.rearrange("(kc p) n -> p kc n", p=128)
    ei = 0

    def loadw(g):
        wg, wgt = wgp.next()
        k.dma('pool', wg[:], w_v[:, :, g * 1024:(g + 1) * 1024], writes=[wgt])
        return wg, wgt

    def fm_proj(wg, wgt, dst_s, dst_tr, scale):
        nonlocal ei
        for c in range(8):
            for tt in range(NT):
                bk, bt = bank(env)
                sl = slice(tt * TT, (tt + 1) * TT)
                for kc in range(KC):
                    k.op('pe', lambda e, o=bk[:], l=wg[:, kc, c * 128:(c + 1) * 128], r=env.uT[:, kc, sl],
                         st=(kc == 0), sp=(kc == KC - 1): e.matmul(o, l, r, start=st, stop=sp),
                         reads=[wgt, env.uT_tr[kc][tt]], writes=[bt], inc=(kc == KC - 1))
                s_, st_ = stg.next()
                evac(k, ('act', 'dve')[ei % 2], s_[:], bk[:], [bt], [st_], scale=scale)
                ei += 1
                k.dma('sp', dst_s.ap()[c, :, sl], s_[:], reads=[st_], writes=[dst_tr[c][tt]])

    def tm_proj(wg, wgt, chunks, dst_s, dst_tr, col0, mode):
        nonlocal ei
        for ch in chunks:
            lb = ch % 16
            src, src_tr = (uT_rem, uT_rem_tr) if ch < 16 else (env.uT, env.uT_tr)
            vs_, vst_ = vstg.next()
            for hf in range(2):
                bk, bt = bank(env)
                for kc in range(KC):
                    k.op('pe', lambda e, o=bk[:], l=src[:, kc, lb * 128:(lb + 1) * 128],
                         r=wg[:, kc, hf * 512:(hf + 1) * 512], st=(kc == 0), sp=(kc == KC - 1):
                         e.matmul(o, l, r, start=st, stop=sp),
                         reads=[wgt, src_tr[kc][lb // 4]], writes=[bt], inc=(kc == KC - 1))
                o_ap = vs_[:, hf * 512:(hf + 1) * 512]
                if mode == 'kd':
                    k.op('dve', lambda e, o=o_ap, a=bk[:], b_=kdec[:, hf * 512:(hf + 1) * 512]:
                         e.tensor_tensor(out=o, in0=a, in1=b_, op=ALU.mult),
                         reads=[bt, kdec_tr], writes=[vst_])
                elif mode == 'silu':
                    k.op('act', lambda e, o=o_ap, i=bk[:]: e.activation(out=o, in_=i, func=AF.Silu),
                         reads=[bt], writes=[vst_])
                else:
                    evac(k, ('act', 'dve')[ei % 2], o_ap, bk[:], [bt], [vst_])
                    ei += 1
            di = ch if len(dst_tr) == 32 else ch - 16
            k.dma('sp', dst_s.ap()[di, :, col0:col0 + 1024], vs_[:], reads=[vst_], writes=[dst_tr[di]])

    wg, wgt = loadw(0)
    wg2, wgt2 = loadw(1)
    fm_proj(wg, wgt, qT_s, qT_s_tr, None)
    wg, wgt = loadw(2)
    fm_proj(wg2, wgt2, kT_s, kT_s_tr, 1.0 / 16.0)
    tm_proj(wg2, wgt2, range(32), kd_s, kd_s_tr, 0, 'kd')
    wg2, wgt2 = loadw(3)
    tm_proj(wg, wgt, range(32), v_s, v_s_tr, 0, 'v')
    wg, wgt = loadw(4)
    tm_proj(wg2, wgt2, range(32), v_s, v_s_tr, 1024, 'v')
    wg2, wgt2 = loadw(5)
    tm_proj(wg, wgt, range(16, 32), sg_s, sg_s_tr, 0, 'silu')
    tm_proj(wg2, wgt2, range(16, 32), sg_s, sg_s_tr, 1024, 'silu')
    k.release(m1)
    k.release(m)
    m = k.mark()
    state = k.alloc([128, 8, 512], F32, "state")
    state_tr = [Tr() for _ in range(8)]
    sbf = k.alloc([128, 8, 512], BF16, "sbf")
    sbf_tr = [Tr() for _ in range(8)]
    k.op('dve', lambda e: e.memset(state[:], 0.0), writes=state_tr)
    k.op('pool', lambda e: e.memset(sbf[:], 0.0), writes=sbf_tr)
    decT = k.alloc([128, 512], F32, "decT")
    decT_tr = Tr()
    k.dma('sp', decT[:], decT_d.ap(), writes=[decT_tr])
    qdec = k.alloc([128, 8, 128], BF16, "qdec")
    qdec_tr = Tr()
    k.dma('pool', qdec[:], qdec_d.ap(), writes=[qdec_tr])
    wo = k.alloc([128, 16, 1024], BF16, "wo")
    wo_tr = Tr()
    wo_v = wo_d.rearrange("(c p) n -> p c n", p=128)
    for hq in range(4):
        k.dma('pool', wo[:, hq * 4:(hq + 1) * 4, :], wo_v[:, hq * 4:(hq + 1) * 4, :], writes=[wo_tr])
    qnp = Rot(k, [128, 8, 128], BF16, 2, "qn")
    knp = Rot(k, [128, 8, 128], BF16, 2, "kn")
    kdp = Rot(k, [128, 1024], BF16, 2, "kd")
    vp = Rot(k, [128, 2048], BF16, 2, "vn")
    sgp = Rot(k, [128, 2048], BF16, 2, "sg")
    sTp = Rot(k, [128, 512], BF16, 2, "sT")
    qdp = Rot(k, [128, 8, 128], BF16, 2, "qd")
    sqjp = Rot(k, [128, 512], F32, 1, "sqj")
    yp = Rot(k, [128, 2048], BF16, 1, "y")
    ssp = Rot(k, [128, 4], F32, 2, "ss")
    obp = Banks(env, [0, 1, 2, 3])
    xbp = Banks(env, [4, 5, 6, 7])
    cds = [g ** 128 for g in RET_GAMMA]

    def state_update(kd, kdt, vn, vnt):
        for h in range(4):
            for dc in range(2):
                i8 = h * 2 + dc
                bk, bt = xbp.next()
                k.op('pe', lambda e, o=bk[:], l=kd[:, h * 256 + dc * 128:h * 256 + (dc + 1) * 128],
                     r=vn[:, h * 512:(h + 1) * 512]: e.matmul(o, l, r, start=True, stop=True),
                     reads=[kdt, vnt], writes=[bt])
                k.op('dve', lambda e, o=state[:, i8, :], i=bk[:], cd=cds[h]:
                     e.scalar_tensor_tensor(out=o, in0=o, scalar=cd, in1=i, op0=ALU.mult, op1=ALU.add),
                     reads=[bt, state_tr[i8]], writes=[state_tr[i8]])
                k.op('act', lambda e, o=sbf[:, i8, :], i=state[:, i8, :]: e.activation(out=o, in_=i, func=AF.Copy),
                     reads=[state_tr[i8]], writes=[sbf_tr[i8]])

    for ch in range(32):
        kd, kdt = kdp.next()
        vn, vnt = vp.next()
        k.dma('sp', kd[:], kd_s.ap()[ch], reads=[kd_s_tr[ch]], writes=[kdt])
        k.dma('sp', vn[:], v_s.ap()[ch], reads=[v_s_tr[ch]], writes=[vnt])
        if ch < 16:
            state_update(kd, kdt, vn, vnt)
            continue
        n = ch - 16
        tt = n // 4
        csl = slice(n * 128, (n + 1) * 128)
        qn, qnt = qnp.next()
        kn, knt = knp.next()
        sg, sgt = sgp.next()
        k.dma('sp', qn[:], qT_s.ap()[:, :, csl].rearrange("c p t -> p c t"), reads=[qT_s_tr[c][tt] for c in range(8)],
              writes=[qnt])
        k.dma('sp', kn[:], kT_s.ap()[:, :, csl].rearrange("c p t -> p c t"), reads=[kT_s_tr[c][tt] for c in range(8)],
              writes=[knt])
        k.dma('sp', sg[:], sg_s.ap()[n], reads=[sg_s_tr[n]], writes=[sgt])
        sb_, sbt = xbp.next()
        for h in range(4):
            for dc in range(2):
                k.op('pe', lambda e, o=sb_[:, h * 128:(h + 1) * 128], l=kn[:, h * 2 + dc, :], r=qn[:, h * 2 + dc, :],
                     st=(dc == 0), sp=(dc == 1): e.matmul(o, l, r, start=st, stop=sp),
                     reads=[knt, qnt], writes=[sbt], inc=(h == 3 and dc == 1))
        sT, sTt = sTp.next()
        k.op('dve', lambda e, o=sT[:], a=sb_[:], b_=decT[:]: e.tensor_tensor(out=o, in0=a, in1=b_, op=ALU.mult),
             reads=[sbt, decT_tr], writes=[sTt])
        qd, qdt = qdp.next()
        k.op('pool', lambda e, o=qd[:], a=qn[:], b_=qdec[:]: e.tensor_tensor(out=o, in0=a, in1=b_, op=ALU.mult),
             reads=[qnt, qdec_tr], writes=[qdt])
        ss, sst = ssp.next()
        obs = []
        for h in range(4):
            ob, obt = obp.next()
            obs.append((ob, obt))
            k.op('pe', lambda e, o=ob[:], l=sT[:, h * 128:(h + 1) * 128], r=vn[:, h * 512:(h + 1) * 512]:
                 e.matmul(o, l, r, start=True, stop=False), reads=[sTt, vnt], writes=[obt], inc=False)
            for dc in range(2):
                i8 = h * 2 + dc
                k.op('pe', lambda e, o=ob[:], l=qd[:, i8, :], r=sbf[:, i8, :], sp=(dc == 1):
                     e.matmul(o, l, r, start=False, stop=sp), reads=[qdt, sbf_tr[i8]], writes=[obt], inc=(dc == 1))
            sqj, sqjt = sqjp.next()
            k.op('act', lambda e, o=sqj[:], i=ob[:]: e.activation(out=o, in_=i, func=AF.Square),
                 reads=[obt], writes=[sqjt])
            k.op('dve', lambda e, o=ss[:, h:h + 1], i=sqj[:]: e.reduce_sum(out=o, in_=i, axis=AX.X),
                 reads=[sqjt], writes=[sst])
        k.op('dve', lambda e, o=ss[:]: e.tensor_scalar(out=o, in0=o, scalar1=1.0 / 512.0, scalar2=EPS,
                                                       op0=ALU.mult, op1=ALU.add), reads=[sst], writes=[sst])
        k.op('act', lambda e, o=ss[:]: e.activation(out=o, in_=o, func=AF.Sqrt), reads=[sst], writes=[sst])
        k.op('dve', lambda e, o=ss[:]: e.reciprocal(out=o, in_=o), reads=[sst], writes=[sst])
        y, yt = yp.next()
        for h in range(4):
            ob, obt = obs[h]
            k.op('dve', lambda e, o=y[:, h * 512:(h + 1) * 512], a=ob[:], s_=ss[:, h:h + 1], b_=sg[:, h * 512:(h + 1) * 512]:
                 e.scalar_tensor_tensor(out=o, in0=a, scalar=s_, in1=b_, op0=ALU.mult, op1=ALU.mult),
                 reads=[obt, sst, sgt], writes=[yt])
        for half in range(2):
            tb, tbt = xbp.next()
            tbb = tb.bitcast(BF16)
            for i in range(8):
                yc = half * 8 + i
                k.op('pe', lambda e, o=tbb[:, i * 128:(i + 1) * 128], i_=y[:, yc * 128:(yc + 1) * 128]:
                     e.transpose(o, i_, env.ident), reads=[yt, env.cm_tr], writes=[tbt], inc=(i == 7))
            for i in range(8):
                yc = half * 8 + i
                col = (yc % 2) * 1024 + (tt % 2) * 512 + (n % 4) * 128
                evac(k, ('act', 'dve')[half], env.uT[:, yc // 2, col:col + 128], tbb[:, i * 128:(i + 1) * 128],
                     [tbt], [env.uT_tr[yc // 2][(col // 512)]])
        state_update(kd, kdt, vn, vnt)
        if n % 4 == 3:
            sl = slice(tt * TT, (tt + 1) * TT)
            for dc in range(KC):
                bk, bt = xbp.next()
                for yc in range(16):
                    col = (yc % 2) * 1024 + (tt % 2) * 512
                    k.op('pe', lambda e, o=bk[:], l=wo[:, yc, dc * 128:(dc + 1) * 128],
                         r=env.uT[:, yc // 2, col:col + 512], st=(yc == 0), sp=(yc == 15):
                         e.matmul(o, l, r, start=st, stop=sp),
                         reads=[wo_tr, env.uT_tr[yc // 2][col // 512]], writes=[bt], inc=(yc == 15))
                k.op('dve', lambda e, o=env.hT[:, dc, sl], i=bk[:]: e.tensor_tensor(out=o, in0=o, in1=i, op=ALU.add),
                     reads=[bt, env.hT_tr[dc][tt]], writes=[env.hT_tr[dc][tt]])
    k.release(m)


def make_ret_tables():
    lg = np.log1p(-np.exp2(-5.0 - np.arange(4, dtype=np.float64)))
    j = np.arange(128)[:, None, None]
    i = np.arange(128)[None, None, :]
    h = np.arange(4)[None, :, None]
    diff = i - j
    decT = np.where(diff >= 0, np.exp(lg[h] * np.maximum(diff, 0)), 0.0)
    decT = np.ascontiguousarray(decT.reshape(128, 512).astype(np.float32))
    qd = np.exp(lg[:, None] * (np.arange(128) + 1.0)[None, :])
    qdec = np.broadcast_to(np.repeat(qd, 2, axis=0)[None, :, :], (128, 8, 128))
    qdec = np.ascontiguousarray(qdec.astype(np.float32))
    kd = np.exp(lg[None, :] * (127.0 - np.arange(128))[:, None]) / 16.0
    kdec = np.ascontiguousarray(np.repeat(kd, 256, axis=1).astype(np.float32))
    return decT, qdec, kdec


def _layer_inputs(li, inputs, hT, remT, half, cm, gains):
    kind, j = li % 3, li // 3
    im = {"hin": hT, "rem": remT, "cm": cm, "gains": gains,
          "wup": np.ascontiguousarray(inputs["mlp_w_up"][li]), "wdn": np.ascontiguousarray(inputs["mlp_w_down"][li])}
    if kind == 0:
        E, Ef = make_swa_tables(half == 0)
        im.update({"wqkv": np.ascontiguousarray(inputs["swa_w_qkv"][j]), "wo": np.ascontiguousarray(inputs["swa_w_o"][j]),
                   "sinks2": make_sinks2(np.asarray(inputs["swa_sinks"][j])), "swaE": E, "swaEf": Ef})
    elif kind == 1:
        im.update({"wqkv": np.ascontiguousarray(inputs["sb_w_qkv"][j]), "wo": np.ascontiguousarray(inputs["sb_w_o"][j]),
                   "sbM": make_sb_masks()})
    else:
        decT, qdec, kdec = make_ret_tables()
        im.update({"win": np.ascontiguousarray(inputs["ret_w_in"][j]), "wo": np.ascontiguousarray(inputs["ret_w_o"][j]),
                   "retD": decT, "retQ": qdec, "retK": kdec})
    return im


def kernel(**inputs):
    inputs = {k_: np.asarray(v, dtype=np.float32) for k_, v in inputs.items()}
    x = inputs["x"]
    NCORES = 8
    cm = make_cm()
    gains = make_gains(inputs["attn_norm"], inputs["mlp_norm"], inputs["final_norm"])
    hTs = [np.ascontiguousarray(x[c // 2, (c % 2) * T:(c % 2 + 1) * T, :].T) for c in range(NCORES)]
    for li in range(4):
        kind = li % 3
        nc = build_layer(li, final=(li == 3))
        in_maps = []
        for c in range(NCORES):
            half = c % 2
            R_ = 128 if kind == 0 else T
            if half == 0:
                remT = np.zeros((D, R_), np.float32)
            else:
                remT = np.ascontiguousarray(hTs[c - 1][:, T - R_:])
            in_maps.append(_layer_inputs(li, inputs, hTs[c], remT, half, cm, gains))
        res = run_bass_kernel_spmd(nc, in_maps, core_ids=list(range(NCORES)))
        hTs = [np.asarray(res.results[c]["hout"], dtype=np.float32) for c in range(NCORES)]
    out = np.empty((4, 4096, D), np.float32)
    for c in range(NCORES):
        out[c // 2, (c % 2) * T:(c % 2 + 1) * T, :] = hTs[c].T
    return out
```
